# Optimizing a Trainium2 kernel written in Bass

```python
import math
import jax, jax.numpy as jnp
from jax import lax
import numpy as np

D_MODEL = 4096
BATCH = 2
SEQ = 8192
DEPTH = 2

N_A_LAYERS = DEPTH // 2
N_B_LAYERS = DEPTH - N_A_LAYERS
N_DENSE = (DEPTH + 1) // 2
N_MOE = DEPTH // 2

POOL_WINDOWS = (2, 4, 8, 16)
N_POOL_GROUPS = len(POOL_WINDOWS)
POOL_GROUP_DIM = D_MODEL // N_POOL_GROUPS

N_HEADS = D_MODEL // 256
HEAD_DIM = D_MODEL // N_HEADS // 2
V_DIM = 2 * HEAD_DIM
QK_WIDTH = N_HEADS * 2 * HEAD_DIM
V_WIDTH = N_HEADS * V_DIM
ROT_DIM = HEAD_DIM // 4
ROPE_THETA = 500000.0
Q_BLOCK = 128

D_FF = 256 * ((8 * D_MODEL // 3 + 255) // 256)
N_EXPERTS = 8
TOP_K = 2
D_FF_EXPERT = D_MODEL
MOE_BLOCK = 256

LN_EPS = 1e-5
DEEPNORM_ALPHA = (2.0 * DEPTH) ** 0.25
DEEPNORM_BETA = (8.0 * DEPTH) ** -0.25
ADA_SCALE = 0.1

kernel_name = "yoco_pool_diffattn_moe_block"


def layer_norm(x, g, b):
    xf = x.astype(jnp.float32)
    mu = jnp.mean(xf, axis=-1, keepdims=True)
    var = jnp.mean(jnp.square(xf - mu), axis=-1, keepdims=True)
    y = (xf - mu) * lax.rsqrt(var + LN_EPS) * g.astype(jnp.float32) + b.astype(jnp.float32)
    return y.astype(x.dtype)


def rms_norm(x, g):
    xf = x.astype(jnp.float32)
    y = xf * lax.rsqrt(jnp.mean(xf * xf, axis=-1, keepdims=True) + LN_EPS) * g.astype(jnp.float32)
    return y.astype(x.dtype)


def modulate(x, shift, scale):
    return x * (1 + scale[:, None, :]) + shift[:, None, :]


def rope_tables(positions):
    inv_freq = ROPE_THETA ** (-jnp.arange(0, ROT_DIM, 2, dtype=jnp.float32) / ROT_DIM)
    ang = positions.astype(jnp.float32)[..., None] * inv_freq
    return jnp.cos(ang), jnp.sin(ang)


def partial_rope(x, cos, sin):
    xr = x[..., :ROT_DIM].astype(jnp.float32)
    x1, x2 = jnp.split(xr, 2, axis=-1)
    c = cos[:, :, None, None, :]
    s = sin[:, :, None, None, :]
    rot = jnp.concatenate([x1 * c - x2 * s, x2 * c + x1 * s], axis=-1).astype(x.dtype)
    return jnp.concatenate([rot, x[..., ROT_DIM:]], axis=-1)


def lambda_init_fn(layer_idx):
    return 0.8 - 0.6 * math.exp(-0.3 * layer_idx)


def multiscale_pool_mixer(h, w_pool, pool_scale):
    b_, s_, d_ = h.shape
    hf = h.astype(jnp.float32).reshape(b_, s_, N_POOL_GROUPS, POOL_GROUP_DIM)
    csum = jnp.cumsum(hf, axis=1)
    t = jnp.arange(1, s_ + 1, dtype=jnp.float32)
    outs = []
    for g, w in enumerate(POOL_WINDOWS):
        cg = csum[:, :, g]
        lagged = jnp.pad(cg, ((0, 0), (w, 0), (0, 0)))[:, :s_]
        mean = (cg - lagged) / jnp.minimum(t, float(w))[None, :, None]
        outs.append(mean - hf[:, :, g])
    pooled = jnp.stack(outs, axis=2).astype(h.dtype)
    mixed = jnp.einsum('bsgc,gcd->bsgd', pooled, w_pool)
    return mixed.reshape(b_, s_, d_) * pool_scale


def shared_kv(x, cond, kv_ada_w, kv_ada_b, w_kv, cos, sin):
    b_, s_, _ = x.shape
    sh, sc = jnp.split(cond @ kv_ada_w + kv_ada_b, 2, axis=-1)
    kv = modulate(x, sh, sc) @ w_kv
    k = kv[..., :QK_WIDTH].reshape(b_, s_, N_HEADS, 2, HEAD_DIM)
    v = kv[..., QK_WIDTH:].reshape(b_, s_, N_HEADS, V_DIM)
    return partial_rope(k, cos, sin), v


def diff_attention(h, k, v, cos, sin, w_q, w_o, lam_q1, lam_k1, lam_q2, lam_k2, subln_g, lambda_init):
    b_, s_, _ = h.shape
    q = partial_rope((h @ w_q).reshape(b_, s_, N_HEADS, 2, HEAD_DIM), cos, sin)
    lam = (jnp.exp(jnp.sum(lam_q1.astype(jnp.float32) * lam_k1.astype(jnp.float32)))
           - jnp.exp(jnp.sum(lam_q2.astype(jnp.float32) * lam_k2.astype(jnp.float32)))
           + lambda_init)
    n_blk = s_ // Q_BLOCK
    q_blocks = q.reshape(b_, n_blk, Q_BLOCK, N_HEADS, 2, HEAD_DIM).transpose(1, 0, 2, 3, 4, 5)
    k_pos = jnp.arange(s_)
    scale = HEAD_DIM ** -0.5

    def block(args):
        qb, i = args
        sc = jnp.einsum('bqhcd,bkhcd->bhcqk', qb, k).astype(jnp.float32) * scale
        q_pos = i * Q_BLOCK + jnp.arange(Q_BLOCK)
        causal = k_pos[None, :] <= q_pos[:, None]
        p = jax.nn.softmax(jnp.where(causal, sc, -jnp.inf), axis=-1)
        p = p[:, :, 0] - lam * p[:, :, 1]
        return jnp.einsum('bhqk,bkhe->bqhe', p.astype(v.dtype), v)

    o = lax.map(block, (q_blocks, jnp.arange(n_blk)))
    o = o.transpose(1, 0, 2, 3, 4).reshape(b_, s_, N_HEADS, V_DIM)
    o = rms_norm(o, subln_g) * (1.0 - lambda_init)
    return o.reshape(b_, s_, V_WIDTH) @ w_o


def swiglu(h, w_gu, w_down):
    g, u = jnp.split(h @ w_gu, 2, axis=-1)
    return (jax.nn.silu(g) * u) @ w_down


def moe_ffn(h, w_router, w_gu_e, w_down_e):
    b_, s_, d_ = h.shape
    n_tok = b_ * s_
    xt = h.reshape(n_tok, d_)
    logits = (xt @ w_router).astype(jnp.float32)
    top_logit, top_idx = lax.top_k(logits, TOP_K)
    top_w = jax.nn.softmax(top_logit, axis=-1)
    n_asg = n_tok * TOP_K
    e_flat = top_idx.reshape(n_asg)
    tok_flat = jnp.arange(n_asg, dtype=jnp.int32) // TOP_K
    w_flat = top_w.reshape(n_asg)
    order = jnp.argsort(e_flat)
    e_sorted = e_flat[order]
    counts = jnp.zeros((N_EXPERTS,), jnp.int32).at[e_flat].add(1)
    starts = jnp.cumsum(counts) - counts
    padded = (counts + MOE_BLOCK - 1) // MOE_BLOCK * MOE_BLOCK
    padded_ends = jnp.cumsum(padded)
    padded_starts = padded_ends - padded
    dest = padded_starts[e_sorted] + (jnp.arange(n_asg, dtype=jnp.int32) - starts[e_sorted])
    n_rows = ((n_asg + MOE_BLOCK - 1) // MOE_BLOCK + N_EXPERTS) * MOE_BLOCK
    row_tok = jnp.full((n_rows,), n_tok, jnp.int32).at[dest].set(tok_flat[order])
    row_w = jnp.zeros((n_rows,), jnp.float32).at[dest].set(w_flat[order])
    n_blk = n_rows // MOE_BLOCK
    blk_start = jnp.arange(n_blk, dtype=jnp.int32) * MOE_BLOCK
    blk_expert = jnp.minimum(jnp.searchsorted(padded_ends, blk_start, side='right'), N_EXPERTS - 1)
    x_pad = jnp.concatenate([xt, jnp.zeros((1, d_), xt.dtype)], axis=0)
    x_rows = x_pad[row_tok].reshape(n_blk, MOE_BLOCK, d_)

    def expert_block(args):
        xb, e = args
        return swiglu(xb, w_gu_e[e], w_down_e[e])

    y_rows = lax.map(expert_block, (x_rows, blk_expert)).reshape(n_rows, d_)
    y_rows = y_rows * row_w[:, None].astype(y_rows.dtype)
    y = jnp.zeros((n_tok + 1, d_), y_rows.dtype).at[row_tok].add(y_rows)[:n_tok]
    return y.reshape(b_, s_, d_)


def setup_inputs(seed: int = 0) -> dict:
    key = jax.random.key(seed)
    ks = jax.random.split(key, 32)
    nrm = jax.random.normal
    D = D_MODEL
    x = nrm(ks[0], (BATCH, SEQ, D), jnp.float32)
    c = nrm(ks[1], (BATCH, D), jnp.float32)
    offset = jax.random.randint(ks[2], (BATCH, 1), 0, 1024, dtype=jnp.int32)
    positions = (jnp.arange(SEQ, dtype=jnp.int32)[None, :] + offset).astype(jnp.int32)
    ada_w = nrm(ks[3], (DEPTH, D, 6 * D), jnp.float32) * (D ** -0.5 * ADA_SCALE)
    ada_b = nrm(ks[4], (DEPTH, 6 * D), jnp.float32) * 0.01
    ln_g = 1.0 + 0.02 * nrm(ks[5], (DEPTH, 2, D), jnp.float32)
    ln_b = 0.02 * nrm(ks[6], (DEPTH, 2, D), jnp.float32)
    kv_ada_w = nrm(ks[7], (D, 2 * D), jnp.float32) * (D ** -0.5 * ADA_SCALE)
    kv_ada_b = nrm(ks[8], (2 * D,), jnp.float32) * 0.01
    w_pool = nrm(ks[9], (N_A_LAYERS, N_POOL_GROUPS, POOL_GROUP_DIM, POOL_GROUP_DIM), jnp.float32) * (POOL_GROUP_DIM ** -0.5 * DEEPNORM_BETA)
    pool_scale = 1.0 + 0.02 * nrm(ks[10], (N_A_LAYERS, D), jnp.float32)
    w_k = nrm(ks[11], (D, QK_WIDTH), jnp.float32) * D ** -0.5
    w_v = nrm(ks[12], (D, V_WIDTH), jnp.float32) * (D ** -0.5 * DEEPNORM_BETA)
    w_kv = jnp.concatenate([w_k, w_v], axis=1)
    w_q = nrm(ks[13], (N_B_LAYERS, D, QK_WIDTH), jnp.float32) * D ** -0.5
    w_o = nrm(ks[14], (N_B_LAYERS, V_WIDTH, D), jnp.float32) * (V_WIDTH ** -0.5 * DEEPNORM_BETA)
    lam_q1 = 0.1 * nrm(ks[15], (N_B_LAYERS, HEAD_DIM), jnp.float32)
    lam_k1 = 0.1 * nrm(ks[16], (N_B_LAYERS, HEAD_DIM), jnp.float32)
    lam_q2 = 0.1 * nrm(ks[17], (N_B_LAYERS, HEAD_DIM), jnp.float32)
    lam_k2 = 0.1 * nrm(ks[18], (N_B_LAYERS, HEAD_DIM), jnp.float32)
    subln_g = 1.0 + 0.02 * nrm(ks[19], (N_B_LAYERS, V_DIM), jnp.float32)
    ffn_w_gu = nrm(ks[20], (N_DENSE, D, 2 * D_FF), jnp.float32) * (D ** -0.5 * DEEPNORM_BETA)
    ffn_w_down = nrm(ks[21], (N_DENSE, D_FF, D), jnp.float32) * (D_FF ** -0.5 * DEEPNORM_BETA)
    router_w = nrm(ks[22], (N_MOE, D, N_EXPERTS), jnp.float32) * D ** -0.5
    moe_w_gu = nrm(ks[23], (N_MOE, N_EXPERTS, D, 2 * D_FF_EXPERT), jnp.float32) * (D ** -0.5 * DEEPNORM_BETA)
    moe_w_down = nrm(ks[24], (N_MOE, N_EXPERTS, D_FF_EXPERT, D), jnp.float32) * (D_FF_EXPERT ** -0.5 * DEEPNORM_BETA)
    return {"x": x, "c": c, "positions": positions, "ada_w": ada_w, "ada_b": ada_b,
            "ln_g": ln_g, "ln_b": ln_b, "kv_ada_w": kv_ada_w, "kv_ada_b": kv_ada_b,
            "w_pool": w_pool, "pool_scale": pool_scale, "w_kv": w_kv, "w_q": w_q, "w_o": w_o,
            "lam_q1": lam_q1, "lam_k1": lam_k1, "lam_q2": lam_q2, "lam_k2": lam_k2,
            "subln_g": subln_g, "ffn_w_gu": ffn_w_gu, "ffn_w_down": ffn_w_down,
            "router_w": router_w, "moe_w_gu": moe_w_gu, "moe_w_down": moe_w_down}


def reference(x, c, positions, ada_w, ada_b, ln_g, ln_b, kv_ada_w, kv_ada_b, w_pool, pool_scale,
              w_kv, w_q, w_o, lam_q1, lam_k1, lam_q2, lam_k2, subln_g, ffn_w_gu, ffn_w_down,
              router_w, moe_w_gu, moe_w_down):
    cos, sin = rope_tables(positions)
    cond = jax.nn.silu(c)
    k_sh = None
    v_sh = None
    for l in range(DEPTH):
        sh_m, sc_m, g_m, sh_f, sc_f, g_f = jnp.split(cond @ ada_w[l] + ada_b[l], 6, axis=-1)
        h = modulate(x, sh_m, sc_m)
        if l < N_A_LAYERS:
            y = multiscale_pool_mixer(h, w_pool[l], pool_scale[l])
        else:
            if l == N_A_LAYERS:
                k_sh, v_sh = shared_kv(x, cond, kv_ada_w, kv_ada_b, w_kv, cos, sin)
            j = l - N_A_LAYERS
            y = diff_attention(h, k_sh, v_sh, cos, sin, w_q[j], w_o[j], lam_q1[j], lam_k1[j],
                               lam_q2[j], lam_k2[j], subln_g[j], lambda_init_fn(l))
        x = layer_norm(DEEPNORM_ALPHA * x + (1 + g_m)[:, None, :] * y, ln_g[l, 0], ln_b[l, 0])
        h = modulate(x, sh_f, sc_f)
        if l % 2 == 0:
            y = swiglu(h, ffn_w_gu[l // 2], ffn_w_down[l // 2])
        else:
            y = moe_ffn(h, router_w[l // 2], moe_w_gu[l // 2], moe_w_down[l // 2])
        x = layer_norm(DEEPNORM_ALPHA * x + (1 + g_f)[:, None, :] * y, ln_g[l, 1], ln_b[l, 1])
    return x
```

```python
import math
from contextlib import ExitStack
import numpy as np
import ml_dtypes
import concourse.bass as bass
import concourse.mybir as mybir
from concourse.bass_utils import run_bass_kernel_spmd

F32 = mybir.dt.float32
BF16 = mybir.dt.bfloat16
I32 = mybir.dt.int32
AF = mybir.ActivationFunctionType
ALU = mybir.AluOpType
AX = mybir.AxisListType

ENGS = ("pe", "act", "dve", "pool", "sp")
ENGS_DMAQ = ("sp", "pool", "act")


class Buf:
    __slots__ = ("name", "w", "r", "rd", "excl")

    def __init__(self, name="", excl=False):
        self.name = name
        self.excl = excl
        self.w = []
        self.r = {}
        self.rd = []


class Op:
    __slots__ = ("id", "eng", "fn", "deps", "dma", "sig", "ndep")

    def __init__(self, id, eng, fn, deps, dma):
        self.id, self.eng, self.fn, self.deps, self.dma = id, eng, fn, deps, dma
        self.sig = None
        self.ndep = 0


class Prog:
    def __init__(self, nc, n_dma_sems=10):
        self.nc = nc
        self.ops = []
        self.n_dma_sems = n_dma_sems
        self.es = ExitStack()
        self.enabled = True
        self.stop = 10 ** 9

    def stage(self, n):
        if n > self.stop:
            self.enabled = False

    def sbuf(self, name, shape, dt):
        return self.es.enter_context(self.nc.sbuf_tensor(name, shape, dt))

    def psum(self, name, shape, dt):
        return self.es.enter_context(self.nc.psum_tensor(name, shape, dt))

    def op(self, eng, fn, reads=(), writes=(), dma=False, extra=(), wadd=()):
        deps = []
        if any(b.excl for b in reads):
            writes = list(writes) + [b for b in reads if b.excl]
            reads = [b for b in reads if not b.excl]
        for b in reads:
            deps.extend(b.w)
        for b in writes:
            deps.extend(b.w)
            deps.extend(b.r.values())
            deps.extend(b.rd)
        deps.extend(extra)
        seen = set()
        d2 = []
        for d in deps:
            if d is None or d.id in seen or d.id < 0:
                continue
            seen.add(d.id)
            d2.append(d)
        if not self.enabled:
            return Op(-1, eng, fn, [], dma)
        o = Op(len(self.ops), eng, fn, d2, dma)
        self.ops.append(o)
        for b in reads:
            if dma:
                b.rd.append(o)
            else:
                b.r[eng] = o
        for b in writes:
            b.w = [o]
            b.r = {}
            b.rd = []
        for b in wadd:
            b.w.append(o)
        return o

    def emit(self):
        nc, ops, es = self.nc, self.ops, self.es
        for o in ops:
            for d in o.deps:
                d.ndep += 1
        sem = {e: es.enter_context(nc.semaphore("s_" + e)) for e in ENGS}
        cnt = {e: 0 for e in ENGS}
        dsem, dcnt, drr = {}, {}, {}
        for e in ENGS_DMAQ:
            dsem[e] = [es.enter_context(nc.semaphore("d_%s%d" % (e, i))) for i in range(self.n_dma_sems)]
            dcnt[e] = [0] * self.n_dma_sems
            drr[e] = 0
        for o in ops:
            if o.dma:
                i = drr[o.eng]
                drr[o.eng] = (i + 1) % self.n_dma_sems
                dcnt[o.eng][i] += 16
                o.sig = (dsem[o.eng][i], dcnt[o.eng][i], ("d", o.eng, i))
            elif o.ndep > 0:
                cnt[o.eng] += 1
                o.sig = (sem[o.eng], cnt[o.eng], ("c", o.eng))
        per = {e: [o for o in ops if o.eng == e] for e in ENGS}
        final = []
        for e in ENGS_DMAQ:
            for i in range(self.n_dma_sems):
                if dcnt[e][i] > 0:
                    final.append((dsem[e][i], dcnt[e][i]))
        self.stats = {e: len(per[e]) for e in ENGS}
        self.maxsem = dict(cnt)
        with nc.Block() as block:
            def run(e, eng):
                known = {}
                for o in per[e]:
                    need = {}
                    for d in o.deps:
                        if d.sig is None:
                            continue
                        s, v, key = d.sig
                        if known.get(key, 0) >= v:
                            continue
                        if key not in need or need[key][1] < v:
                            need[key] = (s, v)
                    if o.dma and o.sig[1] > 16:
                        s, v, key = o.sig
                        if known.get(key, 0) < v - 16 and (key not in need or need[key][1] < v - 16):
                            need[key] = (s, v - 16)
                    for key, (s, v) in need.items():
                        eng.wait_ge(s, v)
                        known[key] = v
                    ins = o.fn(eng)
                    if o.sig is not None:
                        ins.then_inc(o.sig[0], 16 if o.dma else 1)
                if e == "sp":
                    for s, v in final:
                        eng.wait_ge(s, v)

            @block.tensor
            def _(eng):
                run("pe", eng)

            @block.scalar
            def _(eng):
                run("act", eng)

            @block.vector
            def _(eng):
                run("dve", eng)

            @block.gpsimd
            def _(eng):
                run("pool", eng)

            @block.sync
            def _(eng):
                run("sp", eng)
        self.es.close()


class Cfg:
    def __init__(self, D=4096, S=8192):
        self.D, self.S = D, S
        self.T = 512
        self.KC = D // 128
        self.NT = S // self.T
        self.OWN = self.NT // 4
        self.NH = D // 256
        self.DFF = 256 * ((8 * D // 3 + 255) // 256)
        self.FC = self.DFF // 128
        self.NE = 8
        self.DFE = D
        self.NKB = S // 128
        self.PG = D // 4
        self.CG = self.PG // 128
        self.KP = min(8, self.KC)
        self.depth = 2
        self.alpha = (2.0 * self.depth) ** 0.25
        self.lambda_init = 0.8 - 0.6 * math.exp(-0.3 * 1)
        self.eps = 1e-5
        self.XW = self.T + 16
        self.SLOT = max(self.KP * 512, S // 2)
        self.HSZ = max(self.KC * 512, self.NKB * 256)


POOL_WINDOWS = (2, 4, 8, 16)


def build(cfg, stop=10 ** 9):
    D, S, T, KC, NT, OWN, NH, FC, NE, NKB, KP = (cfg.D, cfg.S, cfg.T, cfg.KC, cfg.NT, cfg.OWN, cfg.NH,
                                                 cfg.FC, cfg.NE, cfg.NKB, cfg.KP)
    XW = cfg.XW
    nc = bass.Bass("TRN2", target_bir_lowering=False)

    def din(name, shape, dt=F32):
        return nc.dram_tensor(name, list(shape), dt, kind="ExternalInput").ap()

    x_d = din("x", [NT * T, D])
    halo_d = din("halo", [NT, 16, D])
    tidx_d = din("tidx", [1, NT * T])
    pos_d = din("pos", [1, NT * T], I32)
    c_d = din("c", [D])
    ada_w_d = din("ada_w", [2, D, 6 * D])
    ada_b_d = din("ada_b", [2, 6 * D])
    ln_g_d = din("ln_g", [2, 2, D])
    ln_b_d = din("ln_b", [2, 2, D])
    kv_ada_w_d = din("kv_ada_w", [D, 2 * D])
    kv_ada_b_d = din("kv_ada_b", [2 * D])
    w_pool_d = din("w_pool", [1, 4, cfg.PG, cfg.PG])
    pool_scale_d = din("pool_scale", [1, D])
    w_kv_d = din("w_kv", [D, 2 * D])
    w_q_d = din("w_q", [1, D, D])
    w_o_d = din("w_o", [1, D, D])
    lam_d = [din(n, [1, 128]) for n in ("lam_q1", "lam_k1", "lam_q2", "lam_k2")]
    subln_g_d = din("subln_g", [1, 256])
    ffn_gu_d = din("ffn_w_gu", [1, D, 2 * cfg.DFF])
    ffn_dn_d = din("ffn_w_down", [1, cfg.DFF, D])
    router_d = din("router_w", [1, D, NE])
    moe_gu_d = din("moe_w_gu", [1, NE, D, 2 * cfg.DFE])
    moe_dn_d = din("moe_w_down", [1, NE, cfg.DFE, D])
    consts_d = din("consts", [128, 4 * 128 + 4])
    out_d = nc.dram_tensor("out", [OWN * T, D], F32, kind="ExternalOutput").ap()
    kscr = nc.dram_tensor("kscr", [KC, 128, S], BF16).ap()
    vscr = nc.dram_tensor("vscr", [S, D], BF16).ap()
    x1scr = nc.dram_tensor("x1scr", [OWN, 128, KC, T], F32).ap()

    p = Prog(nc)
    p.stop = stop
    X = p.sbuf("X", [128, KC, XW], F32)
    Xb = [Buf("X%d" % i) for i in range(KC)]
    H = p.sbuf("H", [128, cfg.HSZ], BF16)
    Hv = H[:, 0:KC * 512].rearrange("p (k t) -> p k t", k=KC)
    Hb = [Buf("H%d" % i) for i in range(max(KC, NKB // 2))]
    Vv = H[:, 0:NKB * 256].rearrange("p (k e) -> p k e", k=NKB)
    Pm_ = p.sbuf("P", [128, KC, 512], BF16)
    Pb = [Buf("P%d" % i) for i in range(KC)]
    NSLOT = 4
    WS = [p.sbuf("ws%d" % i, [128, cfg.SLOT], BF16) for i in range(NSLOT)]
    WSb = [Buf("ws%d" % i) for i in range(NSLOT)]
    WSv = [w[:, 0:KP * 512].rearrange("p (k n) -> p k n", k=KP) for w in WS]
    NF = 8
    Fw = [p.sbuf("f%d" % i, [128, XW], F32) for i in range(NF)]
    Fs = [f[:, 0:512] for f in Fw]
    Fb = [Buf("f%d" % i) for i in range(NF)]
    Pt = [p.sbuf("pt%d" % i, [128, 512], BF16) for i in range(2)]
    Ptb = [Buf() for _ in range(2)]
    Pq = [p.sbuf("pq%d" % i, [128, 512], BF16) for i in range(2)]
    Pqb = [Buf() for _ in range(2)]
    xin = [p.sbuf("xin%d" % i, [128, 4, 128], F32) for i in range(2)]
    xinb = [Buf() for _ in range(2)]
    hin = [p.sbuf("hin%d" % i, [16, 128], F32) for i in range(2)]
    hinb = [Buf() for _ in range(2)]
    NV = 14
    mod = p.sbuf("mod", [128, NV * KC], F32)
    modb = Buf("mod")
    adab = p.sbuf("adab", [128, NV * KC], F32)
    adabb = Buf("adab")
    lnp = p.sbuf("lnp", [128, 8 * KC], F32)
    lnpb = Buf("lnp")
    psc = p.sbuf("psc", [128, KC], F32)
    pscb = Buf("psc")
    NDV = 22
    dv = p.sbuf("dv", [128, NDV * KC], F32)
    dvb = Buf("dv")
    cst = p.sbuf("cst", [128, 4 * 128 + 4], F32)
    cstb = Buf("cst")
    cbf = p.sbuf("cbf", [128, 2 * 128], BF16)
    cbfb = Buf("cbf")
    condT = p.sbuf("condT", [128, KC], BF16)
    condf = p.sbuf("condf", [128, KC], F32)
    condb = Buf("cond")
    lamt = p.sbuf("lamt", [128, 8], F32)
    lamb = Buf("lam")
    sg = p.sbuf("sg", [128, 2], F32)
    sgb = Buf("sg")
    kidx = p.sbuf("kidx", [128, NKB], F32)
    kidxb = Buf("kidx")
    wr = p.sbuf("wr", [128, KC, NE], BF16)
    wrb = Buf("wr")
    gate = p.sbuf("gate", [128, 4, NE], F32)
    gateb = Buf("gate")
    rt = [p.sbuf("rt%d" % i, [128, 16], F32) for i in range(6)]
    rtb = [Buf() for _ in range(6)]
    dg = [p.sbuf("dg%d" % i, [128, 128], F32) for i in range(2)]
    dgb = [Buf() for _ in range(2)]
    posi = p.sbuf("posi", [1, 512], I32)
    posib = Buf()
    PS = [p.psum("ps%d" % i, [128, 512], F32) for i in range(8)]
    PSb = [Buf("ps%d" % i, excl=True) for i in range(8)]

    ident = cst[:, 0:128]
    ones = cst[:, 128:256]
    invf_c = cst[:, 512:513]
    sgnS_c = cst[:, 513:514]
    eps_c = cst[:, 514:515]
    ones_bf = cbf[:, 0:128]
    perm_bf = cbf[:, 128:256]

    def V(i):
        return dv[:, i * KC:(i + 1) * KC]

    (DV_POOLSC, DV_COEF0, DV_G0A, DV_B0A, DV_G0H, DV_B0H, DV_GF0,
     DV_G1A, DV_B1A, DV_G1KV, DV_B1KV, DV_SCM1, DV_GM1, DV_G2A, DV_B2A, DV_G2H, DV_B2H, DV_GF1,
     DV_G3, DV_B3, DV_TMP, DV_TMP2) = range(22)

    p.op("sp", lambda e: e.dma_start(out=cst[:], in_=consts_d), writes=[cstb], dma=True)
    p.op("pool", lambda e: e.dma_start(out=cbf[:], in_=consts_d[:, 128:384]), writes=[cbfb], dma=True)
    p.op("sp", lambda e: e.dma_start(out=condf[:], in_=c_d.rearrange("(c p) -> p c", p=128), allow_slow_non_contiguous=True),
         writes=[condb], dma=True)
    for l in range(2):
        for v in range(6):
            vv = l * 6 + v
            p.op("sp", lambda e, l=l, v=v, vv=vv: e.dma_start(out=adab[:, vv * KC:(vv + 1) * KC],
                                                           in_=ada_b_d[l, v * D:(v + 1) * D].rearrange("(c p) -> p c", p=128),
                                                           allow_slow_non_contiguous=True), wadd=[adabb], dma=True)
    for v in range(2):
        p.op("sp", lambda e, v=v: e.dma_start(out=adab[:, (12 + v) * KC:(13 + v) * KC], in_=kv_ada_b_d[v * D:(v + 1) * D].rearrange("(c p) -> p c", p=128),
                                             allow_slow_non_contiguous=True), wadd=[adabb], dma=True)
    for l in range(2):
        for i in range(2):
            k = (l * 2 + i) * 2
            p.op("sp", lambda e, l=l, i=i, k=k: e.dma_start(out=lnp[:, k * KC:(k + 1) * KC], in_=ln_g_d[l, i].rearrange("(c p) -> p c", p=128),
                                                           allow_slow_non_contiguous=True), wadd=[lnpb], dma=True)
            p.op("sp", lambda e, l=l, i=i, k=k: e.dma_start(out=lnp[:, (k + 1) * KC:(k + 2) * KC], in_=ln_b_d[l, i].rearrange("(c p) -> p c", p=128),
                                                           allow_slow_non_contiguous=True), wadd=[lnpb], dma=True)
    p.op("sp", lambda e: e.dma_start(out=psc[:], in_=pool_scale_d[0].rearrange("(c p) -> p c", p=128), allow_slow_non_contiguous=True),
         writes=[pscb], dma=True)
    for i in range(4):
        p.op("sp", lambda e, i=i: e.dma_start(out=lamt[:, i:i + 1], in_=lam_d[i][0].rearrange("(p o) -> p o", o=1),
                                             allow_slow_non_contiguous=True), wadd=[lamb], dma=True)
    p.op("sp", lambda e: e.dma_start(out=sg[:], in_=subln_g_d[0].rearrange("(c p) -> p c", p=128), allow_slow_non_contiguous=True),
         writes=[sgb], dma=True)
    for k0 in range(0, NKB, 16):
        k1 = min(NKB, k0 + 16)
        p.op("sp", lambda e, k0=k0, k1=k1: e.dma_start(out=kidx[:, k0:k1], in_=tidx_d[0, k0 * 128:k1 * 128].rearrange("(k p) -> p k", p=128),
                                                       allow_slow_non_contiguous=True), wadd=[kidxb], dma=True)
    p.op("pool", lambda e: e.dma_start(out=wr[:], in_=router_d[0].rearrange("(c p) n -> p c n", p=128), allow_slow_non_contiguous=True),
         writes=[wrb], dma=True)

    def lnv(l, i, gb):
        k = (l * 2 + i) * 2 + gb
        return lnp[:, k * KC:(k + 1) * KC]

    p.op("act", lambda e: e.activation(out=condT[:], in_=condf[:], func=AF.Silu), reads=[condb], writes=[condb])

    p.op("dve", lambda e: e.tensor_tensor(out=lamt[:, 4:5], in0=lamt[:, 0:1], in1=lamt[:, 1:2], op=ALU.mult), reads=[lamb], writes=[lamb])
    p.op("dve", lambda e: e.tensor_tensor(out=lamt[:, 5:6], in0=lamt[:, 2:3], in1=lamt[:, 3:4], op=ALU.mult), reads=[lamb], writes=[lamb])
    p.op("pe", lambda e: e.matmul(PS[6][:, 0:2], lhsT=ones, rhs=lamt[:, 4:6], start=True, stop=True), reads=[lamb, cstb], writes=[PSb[6]])
    p.op("act", lambda e: e.activation(out=lamt[:, 4:6], in_=PS[6][:, 0:2], func=AF.Exp), reads=[PSb[6]], writes=[lamb])
    p.op("dve", lambda e: e.tensor_tensor(out=lamt[:, 6:7], in0=lamt[:, 4:5], in1=lamt[:, 5:6], op=ALU.subtract), reads=[lamb], writes=[lamb])
    p.op("dve", lambda e: e.tensor_scalar(out=lamt[:, 6:7], in0=lamt[:, 6:7], scalar1=float(cfg.lambda_init), scalar2=None, op0=ALU.add),
         reads=[lamb], writes=[lamb])
    lam_c = lamt[:, 6:7]
    p.op("dve", lambda e: e.tensor_scalar(out=sg[:], in0=sg[:], scalar1=float(1.0 - cfg.lambda_init), scalar2=None, op0=ALU.mult),
         reads=[sgb], writes=[sgb])

    wctr = [0]

    def wload(pieces):
        s = wctr[0] % NSLOT
        wctr[0] += 1
        first = True
        for ap, off in pieces:
            kp_, n_ = ap.shape[1], ap.shape[2]
            if first:
                p.op("pool", lambda e, ap=ap, off=off, s=s, kp_=kp_, n_=n_: e.dma_start(out=WSv[s][:, 0:kp_, off:off + n_], in_=ap),
                     writes=[WSb[s]], dma=True)
                first = False
            else:
                p.op("pool", lambda e, ap=ap, off=off, s=s, kp_=kp_, n_=n_: e.dma_start(out=WSv[s][:, 0:kp_, off:off + n_], in_=ap),
                     wadd=[WSb[s]], dma=True)
        return s

    def linear(src, kcs, slabs, kind, act_in, epi, banks=(0, 1, 2, 3)):
        srcv = src.rearrange("(kc p) n -> p kc n", p=128)
        parts = [kcs[i:i + KP] for i in range(0, len(kcs), KP)]
        for si, pieces in enumerate(slabs):
            tot = sum(n for _, n in pieces)
            nacc = {"W": tot // 128, "X": 4, "R": 1}[kind]
            jbase = 0
            for pi, part in enumerate(parts):
                k0, k1 = part[0], part[-1] + 1
                off = 0
                pl = []
                for c0, n in pieces:
                    pl.append((srcv[:, k0:k1, c0:c0 + n], off))
                    off += n
                s = wload(pl)
                for a in range(nacc):
                    bk = banks[a]
                    for kk, kc in enumerate(part):
                        j = jbase + kk
                        ap, ab = act_in(j)
                        st = (pi == 0 and kk == 0)
                        sp_ = (pi == len(parts) - 1 and kk == len(part) - 1)
                        if kind == "W":
                            fn = lambda e, bk=bk, s=s, kk=kk, a=a, ap=ap, st=st, sp_=sp_: e.matmul(
                                PS[bk][:, :], lhsT=WSv[s][:, kk, a * 128:(a + 1) * 128], rhs=ap, start=st, stop=sp_)
                        elif kind == "X":
                            fn = lambda e, bk=bk, s=s, kk=kk, a=a, ap=ap, st=st, sp_=sp_, tot=tot: e.matmul(
                                PS[bk][:, 0:tot], lhsT=ap[:, a * 128:(a + 1) * 128], rhs=WSv[s][:, kk, 0:tot], start=st, stop=sp_)
                        else:
                            fn = lambda e, bk=bk, s=s, kk=kk, ap=ap, st=st, sp_=sp_, tot=tot: e.matmul(
                                PS[bk][0:1, 0:tot], lhsT=ap, rhs=WSv[s][:, kk, 0:tot], start=st, stop=sp_)
                        o = p.op("pe", fn, reads=[WSb[s], ab], writes=[PSb[bk]] if st else [])
                        if sp_:
                            PSb[bk].w = [o]
                jbase += len(part)
            for a in range(nacc):
                epi(si, a, banks[a])

    ada_srcs = [(ada_w_d[0], 6 * D, 0), (ada_w_d[1], 6 * D, 6), (kv_ada_w_d, 2 * D, 12)]
    rowsel = [0]
    for src, ncols, v0 in ada_srcs:
        slabs = [[(c0, min(512, ncols - c0))] for c0 in range(0, ncols, 512)]

        def epi(si, a, bk, v0=v0, slabs=slabs):
            c0, n = slabs[si][0]
            r = rowsel[0] % 2
            rowsel[0] += 1
            p.op("act", lambda e, r=r, bk=bk, n=n: e.activation(out=Fs[r][0:1, 0:n], in_=PS[bk][0:1, 0:n], func=AF.Identity),
                 reads=[PSb[bk]], writes=[Fb[r]])
            for j in range(n // 128):
                col = v0 * KC + (c0 // 128) + j
                p.op("pe", lambda e, r=r, j=j, col=col: e.matmul(PS[7][:, col:col + 1], lhsT=Fs[r][0:1, j * 128:(j + 1) * 128],
                                                               rhs=ones[0:1, 0:1], start=True, stop=True),
                     reads=[Fb[r], cstb], wadd=[PSb[7]])

        linear(src, list(range(KC)), slabs, "R", lambda j: (condT[:, j:j + 1], condb), epi, banks=(0,))
    p.op("dve", lambda e: e.tensor_tensor(out=mod[:], in0=PS[7][:, 0:NV * KC], in1=adab[:], op=ALU.add),
         reads=[PSb[7], adabb], writes=[modb])

    def M(v):
        return mod[:, v * KC:(v + 1) * KC]

    def dvop(fn, extra_reads=()):
        p.op("dve", fn, reads=[modb, lnpb, pscb, dvb] + list(extra_reads), writes=[dvb])

    def d_add1(dst, src):
        dvop(lambda e: e.tensor_scalar(out=V(dst), in0=src, scalar1=1.0, scalar2=None, op0=ALU.add))

    def d_mul(dst, a, b):
        dvop(lambda e: e.tensor_tensor(out=V(dst), in0=a, in1=b, op=ALU.mult))

    def d_add(dst, a, b):
        dvop(lambda e: e.tensor_tensor(out=V(dst), in0=a, in1=b, op=ALU.add))

    def d_scale(dst, a, s):
        dvop(lambda e: e.tensor_scalar(out=V(dst), in0=a, scalar1=float(s), scalar2=None, op0=ALU.mult))

    al = cfg.alpha
    d_add1(DV_POOLSC, M(1))
    d_add1(DV_TMP, M(2))
    d_mul(DV_COEF0, V(DV_TMP), psc[:])
    d_scale(DV_G0A, lnv(0, 0, 0), al)
    d_scale(DV_B0A, lnv(0, 0, 1), al)
    d_add1(DV_TMP, M(4))
    d_mul(DV_G0H, lnv(0, 0, 0), V(DV_TMP))
    d_mul(DV_TMP2, lnv(0, 0, 1), V(DV_TMP))
    d_add(DV_B0H, V(DV_TMP2), M(3))
    d_add1(DV_GF0, M(5))
    d_scale(DV_G1A, lnv(0, 1, 0), 1.0)
    d_scale(DV_B1A, lnv(0, 1, 1), 1.0)
    d_add1(DV_TMP, M(13))
    d_mul(DV_G1KV, lnv(0, 1, 0), V(DV_TMP))
    d_mul(DV_TMP2, lnv(0, 1, 1), V(DV_TMP))
    d_add(DV_B1KV, V(DV_TMP2), M(12))
    d_add1(DV_SCM1, M(7))
    d_add1(DV_GM1, M(8))
    d_scale(DV_G2A, lnv(1, 0, 0), al)
    d_scale(DV_B2A, lnv(1, 0, 1), al)
    d_add1(DV_TMP, M(10))
    d_mul(DV_G2H, lnv(1, 0, 0), V(DV_TMP))
    d_mul(DV_TMP2, lnv(1, 0, 1), V(DV_TMP))
    d_add(DV_B2H, V(DV_TMP2), M(9))
    d_add1(DV_GF1, M(11))
    d_scale(DV_G3, lnv(1, 1, 0), 1.0)
    d_scale(DV_B3, lnv(1, 1, 1), 1.0)

    def col(vec_idx, c):
        return dv[:, vec_idx * KC + c:vec_idx * KC + c + 1]

    p.stage(1)
    def layer_norm(outs):
        S1, S2 = 4, 5
        for c in range(KC):
            q = c % 2
            p.op("act", lambda e, c=c, q=q: e.activation(out=Fs[q][:], in_=X[:, c, 16:16 + T], func=AF.Square),
                 reads=[Xb[c]], writes=[Fb[q]])
            o = p.op("pe", lambda e, c=c: e.matmul(PS[S1][:], lhsT=ones, rhs=X[:, c, 16:16 + T], start=(c == 0), stop=(c == KC - 1)),
                     reads=[Xb[c], cstb], writes=[PSb[S1]] if c == 0 else [])
            if c == KC - 1:
                PSb[S1].w = [o]
            o = p.op("pe", lambda e, c=c, q=q: e.matmul(PS[S2][:], lhsT=ones, rhs=Fs[q][:], start=(c == 0), stop=(c == KC - 1)),
                     reads=[Fb[q], cstb], writes=[PSb[S2]] if c == 0 else [])
            if c == KC - 1:
                PSb[S2].w = [o]
        mean, rstd, t1, t2 = 4, 5, 2, 3
        p.op("dve", lambda e: e.tensor_scalar(out=Fs[mean][:], in0=PS[S1][:], scalar1=1.0 / D, scalar2=None, op0=ALU.mult),
             reads=[PSb[S1]], writes=[Fb[mean]])
        p.op("dve", lambda e: e.tensor_tensor(out=Fs[t1][:], in0=Fs[mean][:], in1=Fs[mean][:], op=ALU.mult), reads=[Fb[mean]], writes=[Fb[t1]])
        p.op("dve", lambda e: e.scalar_tensor_tensor(out=Fs[t2][:], in0=PS[S2][:], scalar=1.0 / D, in1=Fs[t1][:], op0=ALU.mult, op1=ALU.subtract),
             reads=[PSb[S2], Fb[t1]], writes=[Fb[t2]])
        p.op("act", lambda e: e.activation(out=Fs[t1][:], in_=Fs[t2][:], func=AF.Sqrt, bias=eps_c, scale=1.0), reads=[Fb[t2], cstb], writes=[Fb[t1]])
        p.op("dve", lambda e: e.reciprocal(out=Fs[rstd][:], in_=Fs[t1][:]), reads=[Fb[t1]], writes=[Fb[rstd]])
        for c in range(KC):
            q = c % 2
            p.op("dve", lambda e, c=c, q=q: e.tensor_tensor(out=Fs[q][:], in0=X[:, c, 16:16 + T], in1=Fs[mean][:], op=ALU.subtract),
                 reads=[Xb[c], Fb[mean]], writes=[Fb[q]])
            p.op("dve", lambda e, q=q: e.tensor_tensor(out=Fs[q][:], in0=Fs[q][:], in1=Fs[rstd][:], op=ALU.mult),
                 reads=[Fb[q], Fb[rstd]], writes=[Fb[q]])
            for dst_fn, gv, bv in outs:
                ap, b = dst_fn(c)
                p.op("act", lambda e, ap=ap, q=q, gv=gv, bv=bv, c=c: e.activation(out=ap, in_=Fs[q][:], func=AF.Identity,
                                                                                 bias=col(bv, c), scale=col(gv, c)),
                     reads=[Fb[q], dvb], writes=[b])

    def Xdst(c):
        return X[:, c, 16:16 + T], Xb[c]

    def Hdst(c):
        return Hv[:, c, :], Hb[c]

    def rope_tables(tok0):
        p.op("sp", lambda e: e.dma_start(out=posi[:], in_=pos_d[:, tok0:tok0 + T]), writes=[posib], dma=True)
        p.op("dve", lambda e: e.tensor_copy(out=Fs[0][0:1, :], in_=posi[:]), reads=[posib], writes=[Fb[0]])
        p.op("pe", lambda e: e.matmul(PS[6][:], lhsT=ones[0:1, :], rhs=Fs[0][0:1, :], start=True, stop=True), reads=[Fb[0], cstb], writes=[PSb[6]])
        p.op("dve", lambda e: e.tensor_scalar(out=Fs[1][:], in0=PS[6][:], scalar1=invf_c, scalar2=None, op0=ALU.mult),
             reads=[PSb[6], cstb], writes=[Fb[1]])
        MAGIC = 12582912.0
        C1 = 6.28125
        C2 = 2.0 * math.pi - 6.28125

        def reduce_to_pi(src, kk, dst):
            p.op("dve", lambda e: e.tensor_scalar(out=Fs[kk][:], in0=Fs[src][:], scalar1=1.0 / (2.0 * math.pi), scalar2=MAGIC, op0=ALU.mult, op1=ALU.add),
                 reads=[Fb[src]], writes=[Fb[kk]])
            p.op("dve", lambda e: e.tensor_scalar(out=Fs[kk][:], in0=Fs[kk][:], scalar1=-MAGIC, scalar2=None, op0=ALU.add), reads=[Fb[kk]], writes=[Fb[kk]])
            p.op("dve", lambda e: e.scalar_tensor_tensor(out=Fs[dst][:], in0=Fs[kk][:], scalar=-C1, in1=Fs[src][:], op0=ALU.mult, op1=ALU.add),
                 reads=[Fb[kk], Fb[src]], writes=[Fb[dst]])
            p.op("dve", lambda e: e.scalar_tensor_tensor(out=Fs[dst][:], in0=Fs[kk][:], scalar=-C2, in1=Fs[dst][:], op0=ALU.mult, op1=ALU.add),
                 reads=[Fb[kk], Fb[dst]], writes=[Fb[dst]])
            p.op("dve", lambda e: e.tensor_scalar(out=Fs[dst][:], in0=Fs[dst][:], scalar1=math.pi, scalar2=-math.pi, op0=ALU.min, op1=ALU.max),
                 reads=[Fb[dst]], writes=[Fb[dst]])

        reduce_to_pi(1, 2, 3)
        p.op("act", lambda e: e.activation(out=Fs[7][:], in_=Fs[3][:], func=AF.Sin), reads=[Fb[3]], writes=[Fb[7]])
        p.op("dve", lambda e: e.tensor_scalar(out=Fs[1][:], in0=Fs[1][:], scalar1=0.5 * math.pi, scalar2=None, op0=ALU.add), reads=[Fb[1]], writes=[Fb[1]])
        reduce_to_pi(1, 2, 3)
        p.op("act", lambda e: e.activation(out=Fs[6][:], in_=Fs[3][:], func=AF.Sin), reads=[Fb[3]], writes=[Fb[6]])
        p.op("dve", lambda e: e.tensor_scalar(out=Fs[7][:], in0=Fs[7][:], scalar1=sgnS_c, scalar2=None, op0=ALU.mult), reads=[Fb[7], cstb], writes=[Fb[7]])

    ropectr = [0]

    def rope_epi(bk, dst_ap, dst_buf):
        r = ropectr[0] % 2
        ropectr[0] += 1
        p.stage(7.32 + 0.001 * 1)
        p.op("act", lambda e, r=r, bk=bk: e.activation(out=Pt[r][:], in_=PS[bk][:], func=AF.Identity), reads=[PSb[bk]], writes=[Ptb[r]])
        p.stage(7.32 + 0.001 * 2)
        p.op("pe", lambda e, r=r: e.matmul(PS[6][:], lhsT=perm_bf, rhs=Pt[r][:], start=True, stop=True), reads=[Ptb[r], cbfb], writes=[PSb[6]])
        p.stage(7.32 + 0.001 * 3)
        p.op("dve", lambda e, r=r, bk=bk: e.tensor_tensor(out=Fs[r][:], in0=PS[bk][:], in1=Fs[6][:], op=ALU.mult), reads=[PSb[bk], Fb[6]], writes=[Fb[r]])
        p.stage(7.32 + 0.001 * 4)
        p.op("dve", lambda e, r=r: e.tensor_tensor(out=Fs[2 + r][:], in0=PS[6][:], in1=Fs[7][:], op=ALU.mult), reads=[PSb[6], Fb[7]], writes=[Fb[2 + r]])
        p.stage(7.32 + 0.001 * 5)
        p.op("dve", lambda e, r=r, dst_ap=dst_ap: e.tensor_tensor(out=dst_ap, in0=Fs[r][:], in1=Fs[2 + r][:], op=ALU.add),
             reads=[Fb[r], Fb[2 + r]], writes=[dst_buf])

    def swiglu_ffn(gu_src, dn_src, nfc, gate_fn):
        dff = nfc * 128
        GF = min(nfc, KC)
        groups = [list(range(g0, min(g0 + GF, nfc))) for g0 in range(0, nfc, GF)]
        sctr = [0]
        for grp in groups:
            steps = [grp[i:i + 2] for i in range(0, len(grp), 2)]
            slabs = []
            for st in steps:
                n = len(st) * 128
                slabs.append([(st[0] * 128, n), (dff + st[0] * 128, n)])

            def epi(si, a, bk, steps=steps, grp=grp):
                st = steps[si]
                n = len(st)
                if a < n:
                    return
                j = st[a - n]
                gb = bk - n if False else None
                gbank = (0, 1, 2, 3)[a - n]
                r = sctr[0] % 2
                sctr[0] += 1
                pj = j - grp[0]
                p.op("act", lambda e, r=r, gbank=gbank: e.activation(out=Fs[r][:], in_=PS[gbank][:], func=AF.Silu), reads=[PSb[gbank]], writes=[Fb[r]])
                p.op("dve", lambda e, r=r, bk=bk, pj=pj: e.tensor_tensor(out=Pm_[:, pj, :], in0=Fs[r][:], in1=PS[bk][:], op=ALU.mult),
                     reads=[Fb[r], PSb[bk]], writes=[Pb[pj]])

            linear(gu_src, list(range(KC)), slabs, "W", lambda j: (Hv[:, j, :], Hb[j]), epi)
            slabs_d = [[(c0, min(512, D - c0))] for c0 in range(0, D, 512)]

            def epi_d(si, a, bk):
                gate_fn(si * 4 + a, bk)

            linear(dn_src, grp, slabs_d, "W", lambda j: (Pm_[:, j, :], Pb[j]), epi_d)

    xctr = [0]
    kv_ops = []
    for ti in range(NT):
        tok0 = ti * T
        for c in range(KC):
            r = xctr[0] % 2
            xctr[0] += 1
            p.op("sp", lambda e, r=r, c=c, tok0=tok0: e.dma_start(out=xin[r][:], in_=x_d[tok0:tok0 + T, c * 128:(c + 1) * 128].rearrange("(b p) f -> p b f", p=128)),
                 writes=[xinb[r]], dma=True)
            p.op("sp", lambda e, r=r, c=c, ti=ti: e.dma_start(out=hin[r][:], in_=halo_d[ti, :, c * 128:(c + 1) * 128]), writes=[hinb[r]], dma=True)
            bk = 4 + (c % 2)
            for b in range(4):
                o = p.op("pe", lambda e, r=r, b=b, bk=bk: e.transpose(out=PS[bk][:, b * 128:(b + 1) * 128], in_=xin[r][:, b, :], identity=ident),
                         reads=[xinb[r], cstb], writes=[PSb[bk]] if b == 0 else [])
            PSb[bk].w = [o]
            p.op("pe", lambda e, r=r: e.transpose(out=PS[6][:, 0:16], in_=hin[r][:, :], identity=ident[0:16, 0:16]),
                 reads=[hinb[r], cstb], writes=[PSb[6]])
            p.op("act", lambda e, c=c, bk=bk: e.activation(out=X[:, c, 16:16 + T], in_=PS[bk][:], func=AF.Identity), reads=[PSb[bk]], writes=[Xb[c]])
            p.op("dve", lambda e, c=c: e.tensor_copy(out=X[:, c, 0:16], in_=PS[6][:, 0:16]), reads=[PSb[6]], wadd=[Xb[c]])
        p.stage(2)
        p.op("sp", lambda e, tok0=tok0: e.dma_start(out=Fs[0][0:1, 0:T], in_=tidx_d[:, tok0:tok0 + T]), writes=[Fb[0]], dma=True)
        p.op("pe", lambda e: e.matmul(PS[7][:], lhsT=ones[0:1, :], rhs=Fs[0][0:1, :], start=True, stop=True), reads=[Fb[0], cstb], writes=[PSb[7]])
        p.op("dve", lambda e: e.tensor_scalar(out=Fs[6][:], in0=PS[7][:], scalar1=1.0, scalar2=None, op0=ALU.add), reads=[PSb[7]], writes=[Fb[6]])
        for g, w in enumerate(POOL_WINDOWS):
            p.op("dve", lambda e, w=w: e.tensor_scalar(out=Fs[7][:], in0=Fs[6][:], scalar1=float(w), scalar2=None, op0=ALU.min), reads=[Fb[6]], writes=[Fb[7]])
            p.op("dve", lambda e: e.reciprocal(out=Fs[7][:], in_=Fs[7][:]), reads=[Fb[7]], writes=[Fb[7]])
            for c in range(g * cfg.CG, (g + 1) * cfg.CG):
                cur = None
                step = 1
                lvl = 0
                width = XW
                srcap = lambda lo, hi, c=c: X[:, c, lo:hi]
                while step < w:
                    lo = 2 * step - 1
                    dst = Fw[2 + lvl % 2]
                    if cur is None:
                        p.op("dve", lambda e, c=c, lo=lo, step=step, dst=dst: e.tensor_tensor(out=dst[:, lo:XW], in0=X[:, c, lo:XW], in1=X[:, c, lo - step:XW - step], op=ALU.add),
                             reads=[Xb[c]], writes=[Fb[2 + lvl % 2]])
                    else:
                        p.op("dve", lambda e, lo=lo, step=step, dst=dst, cur=cur: e.tensor_tensor(out=dst[:, lo:XW], in0=cur[:, lo:XW], in1=cur[:, lo - step:XW - step], op=ALU.add),
                             reads=[Fb[2 + (lvl - 1) % 2]], writes=[Fb[2 + lvl % 2]])
                    cur = dst
                    cb = Fb[2 + lvl % 2]
                    step *= 2
                    lvl += 1
                q = c % 2
                p.op("dve", lambda e, cur=cur, q=q: e.tensor_tensor(out=Fs[q][:], in0=cur[:, 16:16 + T], in1=Fs[7][:], op=ALU.mult), reads=[cb, Fb[7]], writes=[Fb[q]])
                p.op("dve", lambda e, q=q, c=c: e.tensor_tensor(out=Fs[q][:], in0=Fs[q][:], in1=X[:, c, 16:16 + T], op=ALU.subtract), reads=[Fb[q], Xb[c]], writes=[Fb[q]])
                p.op("act", lambda e, q=q, c=c: e.activation(out=Pm_[:, c, :], in_=Fs[q][:], func=AF.Identity, scale=col(DV_POOLSC, c)),
                     reads=[Fb[q], dvb], writes=[Pb[c]])
        p.stage(3)
        for g in range(4):
            cg = cfg.CG
            ncols = cg * 128
            slabs = [[(c0, min(512, ncols - c0))] for c0 in range(0, ncols, 512)]

            def epi(si, a, bk, g=g, cg=cg):
                oc = g * cg + si * 4 + a
                r = 2 + (oc % 2)
                p.op("act", lambda e, bk=bk, oc=oc, r=r: e.activation(out=Fs[r][:], in_=PS[bk][:], func=AF.Identity, scale=col(DV_COEF0, oc)),
                     reads=[PSb[bk], dvb], writes=[Fb[r]])
                p.op("dve", lambda e, oc=oc, r=r: e.scalar_tensor_tensor(out=X[:, oc, 16:16 + T], in0=X[:, oc, 16:16 + T], scalar=float(al), in1=Fs[r][:],
                                                                        op0=ALU.mult, op1=ALU.add), reads=[Xb[oc], Fb[r]], writes=[Xb[oc]])

            linear(w_pool_d[0, g], list(range(cg)), slabs, "W", lambda j, g=g, cg=cg: (Pm_[:, g * cg + j, :], Pb[g * cg + j]), epi)
        p.stage(4)
        layer_norm([(Xdst, DV_G0A, DV_B0A), (Hdst, DV_G0H, DV_B0H)])
        p.stage(5)
        def gate0(oc, bk):
            p.op("dve", lambda e, oc=oc, bk=bk: e.scalar_tensor_tensor(out=X[:, oc, 16:16 + T], in0=PS[bk][:], scalar=col(DV_GF0, oc), in1=X[:, oc, 16:16 + T],
                                                                      op0=ALU.mult, op1=ALU.add), reads=[PSb[bk], Xb[oc], dvb], writes=[Xb[oc]])
        swiglu_ffn(ffn_gu_d[0], ffn_dn_d[0], FC, gate0)
        p.stage(6)
        layer_norm([(Xdst, DV_G1A, DV_B1A), (Hdst, DV_G1KV, DV_B1KV)])
        if ti < OWN:
            kv_ops.append(p.op("sp", lambda e, ti=ti: e.dma_start(out=x1scr[ti], in_=X[:, :, 16:16 + T]), reads=Xb, dma=True))
        p.stage(7)
        rope_tables(tok0)
        p.stage(7.3)
        slabs = [[(c0, 512)] if D >= 512 else [(c0, D)] for c0 in range(0, D, 512)]

        def epi_k(si, a, bk, tok0=tok0):
            oc = si * 4 + a
            r = oc % 2
            p.stage(7.32)
            rope_epi(bk, Pq[r][:], Pqb[r])
            p.stage(7.35)
            kv_ops.append(p.op("sp", lambda e, oc=oc, r=r, tok0=tok0: e.dma_start(out=kscr[oc, :, tok0:tok0 + T], in_=Pq[r][:]), reads=[Pqb[r]], dma=True))

        linear(w_kv_d[:, 0:D], list(range(KC)), slabs, "W", lambda j: (Hv[:, j, :], Hb[j]), epi_k)

        def epi_v(si, a, bk, tok0=tok0):
            r = a % 2
            c0 = si * 512
            n = min(512, D - c0)
            p.op("act", lambda e, r=r, bk=bk, n=n: e.activation(out=Pq[r][:, 0:n], in_=PS[bk][:, 0:n], func=AF.Identity), reads=[PSb[bk]], writes=[Pqb[r]])
            kv_ops.append(p.op("sp", lambda e, r=r, a=a, c0=c0, n=n, tok0=tok0: e.dma_start(out=vscr[tok0 + a * 128:tok0 + (a + 1) * 128, c0:c0 + n], in_=Pq[r][:, 0:n]),
                 reads=[Pqb[r]], dma=True))

        p.stage(7.6)
        slabs_v = [[(D + c0, min(512, D - c0))] for c0 in range(0, D, 512)]
        linear(w_kv_d, list(range(KC)), slabs_v, "X", lambda j: (Hv[:, j, :], Hb[j]), epi_v)

    kvdone = Buf("kvdone")
    kvdone.w = list(kv_ops)

    p.stage(8)
    scale = 1.0 / math.sqrt(128.0)
    for qi in range(OWN):
        tok0 = qi * T
        p.op("sp", lambda e, qi=qi: e.dma_start(out=X[:, :, 16:16 + T], in_=x1scr[qi]), reads=[kvdone], writes=Xb, dma=True)
        for c in range(KC):
            p.op("act", lambda e, c=c: e.activation(out=Hv[:, c, :], in_=X[:, c, 16:16 + T], func=AF.Identity, bias=mod[:, 6 * KC + c:6 * KC + c + 1],
                                                   scale=col(DV_SCM1, c)), reads=[Xb[c], dvb, modb], writes=[Hb[c]])
        p.stage(9)
        rope_tables(tok0)
        slabs = [[(c0, min(512, D - c0))] for c0 in range(0, D, 512)]

        def epi_q(si, a, bk):
            oc = si * 4 + a
            rope_epi(bk, Pm_[:, oc, :], Pb[oc])

        linear(w_q_d[0], list(range(KC)), slabs, "W", lambda j: (Hv[:, j, :], Hb[j]), epi_q)
        p.stage(10)
        p.op("sp", lambda e, tok0=tok0: e.dma_start(out=Fs[0][0:1, 0:T], in_=tidx_d[:, tok0:tok0 + T]), writes=[Fb[0]], dma=True)
        p.op("pe", lambda e: e.matmul(PS[7][:], lhsT=ones[0:1, :], rhs=Fs[0][0:1, :], start=True, stop=True), reads=[Fb[0], cstb], writes=[PSb[7]])
        p.op("act", lambda e: e.activation(out=Fs[6][:], in_=PS[7][:], func=AF.Identity), reads=[PSb[7]], writes=[Fb[6]])
        half = NKB // 2
        pctr = [0]
        for h in range(NH):
            p.op("sp", lambda e, h=h: e.dma_start(out=Vv[:, :, :], in_=vscr[:, h * 256:(h + 1) * 256].rearrange("(k p) e -> p k e", p=128)),
                 reads=[kvdone], writes=Hb, dma=True)
            for cmp_ in range(2):
                ch = 2 * h + cmp_
                OB = (2 + 2 * cmp_, 3 + 2 * cmp_)
                SB = 6 + cmp_
                slots = []
                for hf in range(2):
                    s = wctr[0] % NSLOT
                    wctr[0] += 1
                    p.op("pool", lambda e, s=s, ch=ch, hf=hf: e.dma_start(out=WS[s][:, 0:S // 2], in_=kscr[ch, :, hf * (S // 2):(hf + 1) * (S // 2)]),
                         reads=[kvdone], writes=[WSb[s]], dma=True)
                    slots.append(s)
                for kb in range(NKB):
                    s = slots[kb // half]
                    ko = (kb % half) * 128
                    sb_ = kb % 2
                    r = pctr[0] % 2
                    pctr[0] += 1
                    p.op("pe", lambda e, s=s, ko=ko, sb_=sb_, ch=ch: e.matmul(PS[sb_][:], lhsT=WS[s][:, ko:ko + 128], rhs=Pm_[:, ch, :], start=True, stop=True),
                         reads=[WSb[s], Pb[ch]], writes=[PSb[sb_]])
                    p.op("act", lambda e, sb_=sb_, r=r: e.activation(out=Pt[r][:], in_=PS[sb_][:], func=AF.Exp, scale=scale), reads=[PSb[sb_]], writes=[Ptb[r]])
                    p.op("dve", lambda e, r=r, kb=kb: e.scalar_tensor_tensor(out=Pq[r][:], in0=Fs[6][:], scalar=kidx[:, kb:kb + 1], in1=Pt[r][:], op0=ALU.is_ge, op1=ALU.mult),
                         reads=[Fb[6], kidxb, Ptb[r]], writes=[Pqb[r]])
                    st, sp_ = (kb == 0), (kb == NKB - 1)
                    for eh in range(2):
                        o = p.op("pe", lambda e, eh=eh, kb=kb, r=r, st=st, sp_=sp_, OB=OB: e.matmul(PS[OB[eh]][:], lhsT=Vv[:, kb, eh * 128:(eh + 1) * 128], rhs=Pq[r][:], start=st, stop=sp_),
                                 reads=[Hb[kb // 2], Pqb[r]], writes=[PSb[OB[eh]]] if st else [])
                        if sp_:
                            PSb[OB[eh]].w = [o]
                    o = p.op("pe", lambda e, r=r, st=st, sp_=sp_, SB=SB: e.matmul(PS[SB][:], lhsT=ones_bf, rhs=Pq[r][:], start=st, stop=sp_),
                             reads=[cbfb, Pqb[r]], writes=[PSb[SB]] if st else [])
                    if sp_:
                        PSb[SB].w = [o]
            p.op("dve", lambda e: e.reciprocal(out=Fs[4][:], in_=PS[6][:]), reads=[PSb[6]], writes=[Fb[4]])
            p.op("dve", lambda e: e.reciprocal(out=Fs[5][:], in_=PS[7][:]), reads=[PSb[7]], writes=[Fb[5]])
            p.op("dve", lambda e: e.tensor_scalar(out=Fs[5][:], in0=Fs[5][:], scalar1=lam_c, scalar2=None, op0=ALU.mult), reads=[Fb[5], lamb], writes=[Fb[5]])
            for eh in range(2):
                p.op("dve", lambda e, eh=eh: e.tensor_tensor(out=Fs[eh][:], in0=PS[2 + eh][:], in1=Fs[4][:], op=ALU.mult), reads=[PSb[2 + eh], Fb[4]], writes=[Fb[eh]])
                p.op("dve", lambda e, eh=eh: e.tensor_tensor(out=Fs[2 + eh][:], in0=PS[4 + eh][:], in1=Fs[5][:], op=ALU.mult), reads=[PSb[4 + eh], Fb[5]], writes=[Fb[2 + eh]])
                p.op("dve", lambda e, eh=eh: e.tensor_tensor(out=Fs[eh][:], in0=Fs[eh][:], in1=Fs[2 + eh][:], op=ALU.subtract), reads=[Fb[eh], Fb[2 + eh]], writes=[Fb[eh]])
                p.op("act", lambda e, eh=eh: e.activation(out=Fs[2 + eh][:], in_=Fs[eh][:], func=AF.Square), reads=[Fb[eh]], writes=[Fb[2 + eh]])
                o = p.op("pe", lambda e, eh=eh: e.matmul(PS[0][:], lhsT=ones, rhs=Fs[2 + eh][:], start=(eh == 0), stop=(eh == 1)),
                         reads=[Fb[2 + eh], cstb], writes=[PSb[0]] if eh == 0 else [])
            PSb[0].w = [o]
            p.op("act", lambda e: e.activation(out=Fs[4][:], in_=PS[0][:], func=AF.Sqrt, bias=eps_c, scale=1.0 / 256.0), reads=[PSb[0], cstb], writes=[Fb[4]])
            p.op("dve", lambda e: e.reciprocal(out=Fs[4][:], in_=Fs[4][:]), reads=[Fb[4]], writes=[Fb[4]])
            for eh in range(2):
                p.op("dve", lambda e, eh=eh: e.tensor_tensor(out=Fs[eh][:], in0=Fs[eh][:], in1=Fs[4][:], op=ALU.mult), reads=[Fb[eh], Fb[4]], writes=[Fb[eh]])
                p.op("act", lambda e, eh=eh, h=h: e.activation(out=Pm_[:, 2 * h + eh, :], in_=Fs[eh][:], func=AF.Identity, scale=sg[:, eh:eh + 1]),
                     reads=[Fb[eh], sgb], writes=[Pb[2 * h + eh]])
        p.stage(11)
        def epi_o(si, a, bk):
            oc = si * 4 + a
            r = oc % 2
            p.op("act", lambda e, bk=bk, oc=oc, r=r: e.activation(out=Fs[r][:], in_=PS[bk][:], func=AF.Identity, scale=col(DV_GM1, oc)),
                 reads=[PSb[bk], dvb], writes=[Fb[r]])
            p.op("dve", lambda e, oc=oc, r=r: e.scalar_tensor_tensor(out=X[:, oc, 16:16 + T], in0=X[:, oc, 16:16 + T], scalar=float(al), in1=Fs[r][:],
                                                                    op0=ALU.mult, op1=ALU.add), reads=[Xb[oc], Fb[r]], writes=[Xb[oc]])

        linear(w_o_d[0], list(range(KC)), slabs, "W", lambda j: (Pm_[:, j, :], Pb[j]), epi_o)
        layer_norm([(Xdst, DV_G2A, DV_B2A), (Hdst, DV_G2H, DV_B2H)])
        p.stage(12)
        for b in range(4):
            for c in range(KC):
                o = p.op("pe", lambda e, b=b, c=c: e.matmul(PS[7][:, b * NE:(b + 1) * NE], lhsT=Hv[:, c, b * 128:(b + 1) * 128], rhs=wr[:, c, :],
                                                           start=(c == 0), stop=(c == KC - 1)),
                         reads=[Hb[c], wrb], writes=[PSb[7]] if (c == 0 and b == 0) else [])
        PSb[7].w = [o]
        for b in range(4):
            lg, m1, eq, l2, m2, ex = (rt[i] for i in range(6))
            p.op("dve", lambda e, b=b: e.tensor_copy(out=rt[0][:, 0:NE], in_=PS[7][:, b * NE:(b + 1) * NE]), reads=[PSb[7]], writes=[rtb[0]])
            p.op("dve", lambda e: e.tensor_reduce(out=rt[1][:, 0:1], in_=rt[0][:, 0:NE], axis=AX.X, op=ALU.max), reads=[rtb[0]], writes=[rtb[1]])
            p.op("dve", lambda e: e.tensor_scalar(out=rt[2][:, 0:NE], in0=rt[0][:, 0:NE], scalar1=rt[1][:, 0:1], scalar2=None, op0=ALU.is_equal),
                 reads=[rtb[0], rtb[1]], writes=[rtb[2]])
            p.op("dve", lambda e: e.scalar_tensor_tensor(out=rt[3][:, 0:NE], in0=rt[2][:, 0:NE], scalar=-1e30, in1=rt[0][:, 0:NE], op0=ALU.mult, op1=ALU.add),
                 reads=[rtb[2], rtb[0]], writes=[rtb[3]])
            p.op("dve", lambda e: e.tensor_reduce(out=rt[4][:, 0:1], in_=rt[3][:, 0:NE], axis=AX.X, op=ALU.max), reads=[rtb[3]], writes=[rtb[4]])
            p.op("dve", lambda e: e.tensor_scalar(out=rt[2][:, 0:NE], in0=rt[0][:, 0:NE], scalar1=rt[4][:, 0:1], scalar2=None, op0=ALU.is_ge),
                 reads=[rtb[0], rtb[4]], writes=[rtb[2]])
            p.op("dve", lambda e: e.tensor_scalar(out=rt[1][:, 1:2], in0=rt[1][:, 0:1], scalar1=-1.0, scalar2=None, op0=ALU.mult), reads=[rtb[1]], writes=[rtb[1]])
            p.op("act", lambda e: e.activation(out=rt[5][:, 0:NE], in_=rt[0][:, 0:NE], func=AF.Exp, bias=rt[1][:, 1:2], scale=1.0), reads=[rtb[0], rtb[1]], writes=[rtb[5]])
            p.op("dve", lambda e: e.tensor_tensor(out=rt[5][:, 0:NE], in0=rt[5][:, 0:NE], in1=rt[2][:, 0:NE], op=ALU.mult), reads=[rtb[5], rtb[2]], writes=[rtb[5]])
            p.op("dve", lambda e: e.tensor_reduce(out=rt[4][:, 1:2], in_=rt[5][:, 0:NE], axis=AX.X, op=ALU.add), reads=[rtb[5]], writes=[rtb[4]])
            p.op("dve", lambda e: e.reciprocal(out=rt[4][:, 2:3], in_=rt[4][:, 1:2]), reads=[rtb[4]], writes=[rtb[4]])
            p.op("dve", lambda e, b=b: e.tensor_scalar(out=gate[:, b, :], in0=rt[5][:, 0:NE], scalar1=rt[4][:, 2:3], scalar2=None, op0=ALU.mult),
                 reads=[rtb[5], rtb[4]], wadd=[gateb])
        p.stage(13)
        for ex_i in range(NE):
            for b in range(4):
                r = b % 2
                p.op("dve", lambda e, r=r, b=b, ex_i=ex_i: e.tensor_scalar(out=dg[r][:], in0=ident, scalar1=gate[:, b, ex_i:ex_i + 1], scalar2=None, op0=ALU.mult),
                     reads=[gateb, cstb], writes=[dgb[r]])
                o = p.op("pe", lambda e, r=r, b=b: e.matmul(PS[7][:, b * 128:(b + 1) * 128], lhsT=ones, rhs=dg[r][:], start=True, stop=True),
                         reads=[dgb[r], cstb], writes=[PSb[7]] if b == 0 else [])
            PSb[7].w = [o]
            p.op("act", lambda e: e.activation(out=Fs[7][:], in_=PS[7][:], func=AF.Identity), reads=[PSb[7]], writes=[Fb[7]])

            def gate1(oc, bk):
                r = 2 + (oc % 2)
                p.op("dve", lambda e, bk=bk, r=r: e.tensor_tensor(out=Fs[r][:], in0=PS[bk][:], in1=Fs[7][:], op=ALU.mult), reads=[PSb[bk], Fb[7]], writes=[Fb[r]])
                p.op("dve", lambda e, oc=oc, r=r: e.scalar_tensor_tensor(out=X[:, oc, 16:16 + T], in0=Fs[r][:], scalar=col(DV_GF1, oc), in1=X[:, oc, 16:16 + T],
                                                                        op0=ALU.mult, op1=ALU.add), reads=[Fb[r], Xb[oc], dvb], writes=[Xb[oc]])

            swiglu_ffn(moe_gu_d[0, ex_i], moe_dn_d[0, ex_i], cfg.DFE // 128, gate1)
        p.stage(14)
        layer_norm([(Xdst, DV_G3, DV_B3)])
        for c in range(KC):
            bk = c % 2
            for b in range(4):
                o = p.op("pe", lambda e, c=c, b=b, bk=bk: e.transpose(out=PS[bk][:, b * 128:(b + 1) * 128], in_=X[:, c, 16 + b * 128:16 + (b + 1) * 128], identity=ident),
                         reads=[Xb[c], cstb], writes=[PSb[bk]] if b == 0 else [])
            PSb[bk].w = [o]
            r = c % 2
            p.op("act", lambda e, r=r, bk=bk: e.activation(out=xin[r][:].rearrange("p b f -> p (b f)"), in_=PS[bk][:], func=AF.Identity), reads=[PSb[bk]], writes=[xinb[r]])
            p.op("sp", lambda e, r=r, c=c, tok0=tok0: e.dma_start(out=out_d[tok0:tok0 + T, c * 128:(c + 1) * 128].rearrange("(b p) f -> p b f", p=128), in_=xin[r][:]),
                 reads=[xinb[r]], dma=True)
    p.emit()
    return nc, p


def make_consts():
    c = np.zeros((128, 4 * 128 + 4), np.float32)
    c[:, 0:128] = np.eye(128, dtype=np.float32)
    c[:, 128:256] = 1.0
    perm = np.zeros((128, 128), np.float32)
    for m in range(32):
        k = m + 16 if m < 16 else m - 16
        perm[k, m] = 1.0
    c[:, 256:384] = perm
    rot = 32
    inv_freq = (500000.0 ** (-np.arange(0, rot, 2, dtype=np.float32) / rot)).astype(np.float32)
    for d in range(32):
        c[d, 512] = inv_freq[d % 16]
        c[d, 513] = -1.0 if d < 16 else 1.0
    c[:, 514] = 1e-5
    return c


def make_in_maps(cfg, inputs):
    S, T, NT, OWN, D = cfg.S, cfg.T, cfg.NT, cfg.OWN, cfg.D
    consts = make_consts()
    x = np.asarray(inputs["x"])
    pos = np.asarray(inputs["positions"])
    cc = np.asarray(inputs["c"])
    shared = {k: np.ascontiguousarray(np.asarray(inputs[k])) for k in
              ("ada_w", "ada_b", "ln_g", "ln_b", "kv_ada_w", "kv_ada_b", "w_pool", "pool_scale", "w_kv", "w_q", "w_o",
               "lam_q1", "lam_k1", "lam_q2", "lam_k2", "subln_g", "ffn_w_gu", "ffn_w_down", "router_w", "moe_w_gu", "moe_w_down")}
    maps = []
    for core in range(8):
        b, j = core // 4, core % 4
        order = list(range(j * OWN, (j + 1) * OWN)) + [t for t in range(NT) if not (j * OWN <= t < (j + 1) * OWN)]
        tokidx = np.concatenate([np.arange(t * T, (t + 1) * T) for t in order])
        xr = np.ascontiguousarray(x[b][tokidx])
        halo = np.zeros((NT, 16, D), np.float32)
        for i, t in enumerate(order):
            if t > 0:
                halo[i] = x[b][t * T - 16:t * T]
        m = dict(shared)
        m["x"] = xr
        m["halo"] = halo
        m["tidx"] = tokidx.astype(np.float32)[None, :]
        m["pos"] = np.ascontiguousarray(pos[b][tokidx].astype(np.int32))[None, :]
        m["c"] = np.ascontiguousarray(cc[b])
        m["consts"] = consts
        maps.append(m)
    return maps


_CACHE = {}


def run(cfg, inputs):
    key = (cfg.D, cfg.S)
    if key not in _CACHE:
        _CACHE[key] = build(cfg)[0]
    nc = _CACHE[key]
    maps = make_in_maps(cfg, inputs)
    res = run_bass_kernel_spmd(nc, maps, core_ids=list(range(8)))
    S, OWN, T, D = cfg.S, cfg.OWN, cfg.T, cfg.D
    out = np.zeros((2, S, D), np.float32)
    for core in range(8):
        b, j = core // 4, core % 4
        out[b, j * OWN * T:(j + 1) * OWN * T] = res.results[core]["out"]
    return out


def kernel(**inputs):
    cfg = Cfg(4096, 8192)
    return run(cfg, inputs)
```
